# Optimizing a Trainium2 kernel written in Bass

```python
import jax
import jax.numpy as jnp
from jax import lax
import numpy as np

D_MODEL = 2048
BATCH = 4
SEQ = 2048
DEPTH = 4

CHUNK = 64
QBLOCK = 128
ROPE_BASE = 10000.0
MAX_POS_OFFSET = 4096
NORM_EPS = 1e-6

RET_HEADS = 8
RET_QK_DIM = 128
RET_V_DIM = 128
RET_WIDTH = RET_HEADS * RET_V_DIM

LRU_WIDTH = 1024
LRU_BLOCKS = 8
LRU_BLOCK_DIM = LRU_WIDTH // LRU_BLOCKS
CONV_WIDTH = 4
LRU_C = 8.0

MLA_HEADS = 8
MLA_NOPE_DIM = 128
MLA_ROPE_DIM = 64
MLA_V_DIM = 128
MLA_Q_LORA = 512
MLA_KV_LORA = 512
MLA_WIDTH = MLA_HEADS * MLA_V_DIM

N_BRANCH = 3
MIX_WIDTH = RET_WIDTH + LRU_WIDTH + MLA_WIDTH
IN_SPLITS = (
    RET_HEADS * RET_QK_DIM,
    RET_HEADS * RET_QK_DIM,
    RET_WIDTH,
    RET_WIDTH,
    LRU_WIDTH,
    LRU_WIDTH,
    MLA_Q_LORA,
    MLA_KV_LORA,
    MLA_ROPE_DIM,
    MLA_WIDTH,
    N_BRANCH * D_MODEL,
)
IN_WIDTH = sum(IN_SPLITS)

kernel_name = 'hybrid_retention_rglru_mla_streaming_block'


def rms_norm(x, gain):
    xf = x.astype(jnp.float32)
    y = xf * lax.rsqrt(jnp.mean(xf * xf, axis=-1, keepdims=True) + NORM_EPS)
    return (y * gain.astype(jnp.float32)).astype(x.dtype)


def rope_tables(positions, dim):
    inv_freq = ROPE_BASE ** (-jnp.arange(0, dim, 2, dtype=jnp.float32) / dim)
    ang = positions.astype(jnp.float32)[:, :, None, None] * inv_freq
    return jnp.cos(ang), jnp.sin(ang)


def apply_rope(x, cos, sin):
    half = x.shape[-1] // 2
    xf = x.astype(jnp.float32)
    x1, x2 = xf[..., :half], xf[..., half:]
    return jnp.concatenate([x1 * cos - x2 * sin, x2 * cos + x1 * sin], axis=-1).astype(x.dtype)


def retention_branch(q, k, v, gate, gn, cos, sin):
    B, S = q.shape[:2]
    NC = S // CHUNK
    q = apply_rope(q.reshape(B, S, RET_HEADS, RET_QK_DIM), cos, sin) * (RET_QK_DIM ** -0.5)
    k = apply_rope(k.reshape(B, S, RET_HEADS, RET_QK_DIM), cos, sin)
    q = q.reshape(B, NC, CHUNK, RET_HEADS, RET_QK_DIM)
    k = k.reshape(B, NC, CHUNK, RET_HEADS, RET_QK_DIM)
    v = v.reshape(B, NC, CHUNK, RET_HEADS, RET_V_DIM)

    log_gamma = jnp.log1p(-jnp.exp2(-5.0 - jnp.arange(RET_HEADS, dtype=jnp.float32)))
    idx = jnp.arange(CHUNK, dtype=jnp.float32)
    intra_decay = jnp.exp(log_gamma[:, None, None] * jnp.abs(idx[:, None] - idx[None, :]))

    scores = jnp.einsum('bnihd,bnjhd->bhnij', q, k) * intra_decay[None, :, None]
    o_intra = jnp.einsum('bhnij,bnjhe->bnihe', scores, v)

    k_dec = k * jnp.exp(log_gamma[None, :] * (CHUNK - 1 - idx)[:, None])[None, None, :, :, None]
    kv_chunk = jnp.einsum('bnjhd,bnjhe->nbhde', k_dec, v)
    chunk_decay = jnp.exp(log_gamma * CHUNK)[None, :, None, None]

    def step(state, kv_n):
        return state * chunk_decay + kv_n, state

    _, prev_state = lax.scan(step, jnp.zeros(kv_chunk.shape[1:], kv_chunk.dtype), kv_chunk)
    q_dec = q * jnp.exp(log_gamma[None, :] * (idx + 1.0)[:, None])[None, None, :, :, None]
    o_inter = jnp.einsum('bnihd,nbhde->bnihe', q_dec, prev_state)

    o = (o_intra + o_inter).reshape(B, S, RET_HEADS, RET_V_DIM).astype(jnp.float32)
    mean = jnp.mean(o, axis=-1, keepdims=True)
    var = jnp.mean(jnp.square(o - mean), axis=-1, keepdims=True)
    o = ((o - mean) * lax.rsqrt(var + NORM_EPS)).reshape(B, S, RET_WIDTH) * gn.astype(jnp.float32)
    return o.astype(gate.dtype) * jax.nn.silu(gate)


def rglru_branch(xb, gate, conv_w, conv_b, wa, ba, wx, bx, lam):
    B, S, W = xb.shape
    xc = lax.conv_general_dilated(
        xb, conv_w[:, None, :].astype(xb.dtype), window_strides=(1,),
        padding=[(CONV_WIDTH - 1, 0)], dimension_numbers=('NWC', 'WIO', 'NWC'),
        feature_group_count=W) + conv_b
    xr = xc.reshape(B, S, LRU_BLOCKS, LRU_BLOCK_DIM)
    r = jax.nn.sigmoid(jnp.einsum('bsnc,ncd->bsnd', xr, wa).reshape(B, S, W) + ba)
    i = jax.nn.sigmoid(jnp.einsum('bsnc,ncd->bsnd', xr, wx).reshape(B, S, W) + bx)
    log_a = -LRU_C * r.astype(jnp.float32) * jax.nn.softplus(-lam.astype(jnp.float32))
    a = jnp.exp(log_a)
    b = jnp.sqrt(-jnp.expm1(2.0 * log_a)) * (i * xc).astype(jnp.float32)

    def combine(left, right):
        a1, b1 = left
        a2, b2 = right
        return a1 * a2, a2 * b1 + b2

    _, h = lax.associative_scan(combine, (a, b), axis=1)
    return h.astype(xb.dtype) * jax.nn.silu(gate)


def mla_branch(q_lat, kv_lat, k_rope, gate, q_norm, w_uq, kv_norm, w_ukv, cos, sin):
    B, S = q_lat.shape[:2]
    q = (rms_norm(q_lat, q_norm) @ w_uq).reshape(B, S, MLA_HEADS, MLA_NOPE_DIM + MLA_ROPE_DIM)
    q_nope = q[..., :MLA_NOPE_DIM]
    q_rope = apply_rope(q[..., MLA_NOPE_DIM:], cos, sin)
    kv = (rms_norm(kv_lat, kv_norm) @ w_ukv).reshape(B, S, MLA_HEADS, MLA_NOPE_DIM + MLA_V_DIM)
    k_nope, v = kv[..., :MLA_NOPE_DIM], kv[..., MLA_NOPE_DIM:]
    k_rope = apply_rope(k_rope[:, :, None, :], cos, sin)[:, :, 0]
    scale = (MLA_NOPE_DIM + MLA_ROPE_DIM) ** -0.5

    outs = []
    for qb in range(S // QBLOCK):
        qs, qe = qb * QBLOCK, (qb + 1) * QBLOCK
        s = (jnp.einsum('bqhd,bkhd->bhqk', q_nope[:, qs:qe], k_nope[:, :qe])
             + jnp.einsum('bqhr,bkr->bhqk', q_rope[:, qs:qe], k_rope[:, :qe]))
        s = s.astype(jnp.float32) * scale
        q_chunk = (qs + jnp.arange(QBLOCK)) // CHUNK
        k_chunk = jnp.arange(qe) // CHUNK
        mask = k_chunk[None, :] <= q_chunk[:, None]
        p = jax.nn.softmax(jnp.where(mask, s, -1e30), axis=-1).astype(v.dtype)
        outs.append(jnp.einsum('bhqk,bkhd->bqhd', p, v[:, :qe]))
    o = jnp.concatenate(outs, axis=1).reshape(B, S, MLA_WIDTH)
    return o * jax.nn.silu(gate)


def hybrid_layer(x, c_act, ada_w, ada_b, norm_pre, norm_post, w_in, ret_gn,
                 lru_conv_w, lru_conv_b, lru_wa, lru_ba, lru_wx, lru_bx, lru_lambda,
                 mla_q_norm, mla_w_uq, mla_kv_norm, mla_w_ukv, w_branch, w_out,
                 cos_ret, sin_ret, cos_mla, sin_mla):
    B, S, _ = x.shape
    mod = c_act @ ada_w + ada_b
    shift, scale, res_gate = jnp.split(mod, 3, axis=-1)
    h = rms_norm(x, norm_pre) * (1.0 + scale[:, None, :]) + shift[:, None, :]

    proj = h @ w_in
    offsets = [int(o) for o in np.cumsum(IN_SPLITS)[:-1]]
    (rq, rk, rv, rg, lx, lg, mq, mkv, mkr, mg, merge_logits) = jnp.split(proj, offsets, axis=-1)

    y_ret = retention_branch(rq, rk, rv, rg, ret_gn, cos_ret, sin_ret)
    y_lru = rglru_branch(lx, lg, lru_conv_w, lru_conv_b, lru_wa, lru_ba, lru_wx, lru_bx, lru_lambda)
    y_mla = mla_branch(mq, mkv, mkr, mg, mla_q_norm, mla_w_uq, mla_kv_norm, mla_w_ukv, cos_mla, sin_mla)

    gates = jax.nn.sigmoid(merge_logits.astype(jnp.float32)).astype(x.dtype).reshape(B, S, N_BRANCH, D_MODEL)
    wb_ret = w_branch[:RET_WIDTH]
    wb_lru = w_branch[RET_WIDTH:RET_WIDTH + LRU_WIDTH]
    wb_mla = w_branch[RET_WIDTH + LRU_WIDTH:]
    merged = (gates[:, :, 0] * (y_ret @ wb_ret)
              + gates[:, :, 1] * (y_lru @ wb_lru)
              + gates[:, :, 2] * (y_mla @ wb_mla))
    y = merged @ w_out
    return x + (1.0 + res_gate[:, None, :]) * rms_norm(y, norm_post)


def setup_inputs(seed: int = 0) -> dict:
    key = jax.random.key(seed)
    ks = jax.random.split(key, 24)
    f32 = jnp.float32

    def nrm(k, shape, s):
        return jax.random.normal(k, shape, f32) * s

    x = nrm(ks[0], (BATCH, SEQ, D_MODEL), 1.0)
    c = nrm(ks[1], (BATCH, D_MODEL), 1.0)
    positions = (jnp.arange(SEQ, dtype=jnp.int32)[None, :]
                 + jax.random.randint(ks[2], (BATCH, 1), 0, MAX_POS_OFFSET, dtype=jnp.int32))
    ada_w = nrm(ks[3], (DEPTH, D_MODEL, 3 * D_MODEL), 0.5 * D_MODEL ** -0.5)
    ada_b = nrm(ks[4], (DEPTH, 3 * D_MODEL), 0.01)
    norm_pre = 1.0 + nrm(ks[5], (DEPTH, D_MODEL), 0.01)
    norm_post = 1.0 + nrm(ks[6], (DEPTH, D_MODEL), 0.01)
    w_in = nrm(ks[7], (DEPTH, D_MODEL, IN_WIDTH), D_MODEL ** -0.5)
    ret_gn = 1.0 + nrm(ks[8], (DEPTH, RET_WIDTH), 0.01)
    lru_conv_w = nrm(ks[9], (DEPTH, CONV_WIDTH, LRU_WIDTH), CONV_WIDTH ** -0.5)
    lru_conv_b = nrm(ks[10], (DEPTH, LRU_WIDTH), 0.01)
    lru_wa = nrm(ks[11], (DEPTH, LRU_BLOCKS, LRU_BLOCK_DIM, LRU_BLOCK_DIM), LRU_BLOCK_DIM ** -0.5)
    lru_ba = nrm(ks[12], (DEPTH, LRU_WIDTH), 0.01)
    lru_wx = nrm(ks[13], (DEPTH, LRU_BLOCKS, LRU_BLOCK_DIM, LRU_BLOCK_DIM), LRU_BLOCK_DIM ** -0.5)
    lru_bx = nrm(ks[14], (DEPTH, LRU_WIDTH), 0.01)
    u = jax.random.uniform(ks[15], (DEPTH, LRU_WIDTH), f32, 0.9, 0.999)
    a0 = u ** (1.0 / LRU_C)
    lru_lambda = jnp.log(a0) - jnp.log1p(-a0)
    mla_q_norm = 1.0 + nrm(ks[16], (DEPTH, MLA_Q_LORA), 0.01)
    mla_w_uq = nrm(ks[17], (DEPTH, MLA_Q_LORA, MLA_HEADS * (MLA_NOPE_DIM + MLA_ROPE_DIM)), MLA_Q_LORA ** -0.5)
    mla_kv_norm = 1.0 + nrm(ks[18], (DEPTH, MLA_KV_LORA), 0.01)
    mla_w_ukv = nrm(ks[19], (DEPTH, MLA_KV_LORA, MLA_HEADS * (MLA_NOPE_DIM + MLA_V_DIM)), MLA_KV_LORA ** -0.5)
    w_branch = nrm(ks[20], (DEPTH, MIX_WIDTH, D_MODEL), (MIX_WIDTH // N_BRANCH) ** -0.5)
    w_out = nrm(ks[21], (DEPTH, D_MODEL, D_MODEL), D_MODEL ** -0.5)
    return {'x': x, 'c': c, 'positions': positions, 'ada_w': ada_w, 'ada_b': ada_b,
            'norm_pre': norm_pre, 'norm_post': norm_post, 'w_in': w_in, 'ret_gn': ret_gn,
            'lru_conv_w': lru_conv_w, 'lru_conv_b': lru_conv_b, 'lru_wa': lru_wa, 'lru_ba': lru_ba,
            'lru_wx': lru_wx, 'lru_bx': lru_bx, 'lru_lambda': lru_lambda,
            'mla_q_norm': mla_q_norm, 'mla_w_uq': mla_w_uq, 'mla_kv_norm': mla_kv_norm,
            'mla_w_ukv': mla_w_ukv, 'w_branch': w_branch, 'w_out': w_out}


def reference(x, c, positions, ada_w, ada_b, norm_pre, norm_post, w_in, ret_gn,
              lru_conv_w, lru_conv_b, lru_wa, lru_ba, lru_wx, lru_bx, lru_lambda,
              mla_q_norm, mla_w_uq, mla_kv_norm, mla_w_ukv, w_branch, w_out):
    c_act = jax.nn.silu(c)
    cos_ret, sin_ret = rope_tables(positions, RET_QK_DIM)
    cos_mla, sin_mla = rope_tables(positions, MLA_ROPE_DIM)
    for l in range(DEPTH):
        x = hybrid_layer(x, c_act, ada_w[l], ada_b[l], norm_pre[l], norm_post[l], w_in[l], ret_gn[l],
                         lru_conv_w[l], lru_conv_b[l], lru_wa[l], lru_ba[l], lru_wx[l], lru_bx[l],
                         lru_lambda[l], mla_q_norm[l], mla_w_uq[l], mla_kv_norm[l], mla_w_ukv[l],
                         w_branch[l], w_out[l], cos_ret, sin_ret, cos_mla, sin_mla)
    return x
```

```python
import contextlib
import numpy as np
import concourse.bass as bass
import concourse.mybir as mybir
from concourse.bass_utils import run_bass_kernel_spmd

F32 = mybir.dt.float32
BF16 = mybir.dt.bfloat16
I32 = mybir.dt.int32
AF = mybir.ActivationFunctionType
ALU = mybir.AluOpType

D = 2048
S = 2048
DEPTH = 4
TT = 512
NPASS = S // TT
INW = 14400
EPS = 1e-6
NV = 160
SAME_ENGINE_SYNC = True
STAGE = 99
NPASS_RUN = None
NTDBG = 1


class Res:
    __slots__ = ("name", "last_w", "reads")

    def __init__(self, name):
        self.name = name
        self.last_w = None
        self.reads = []


class Prog:
    ENGS = ("pe", "act", "dve", "pool", "sp")

    def __init__(self, nc):
        self.nc = nc
        self.ops = {e: [] for e in self.ENGS}
        self.cnt = {e: 0 for e in self.ENGS}
        self.waited = {e: {} for e in self.ENGS}
        self.dma_cnt = {}
        self.sem_keys = []
        self.final_waits = []
        self.seg = 0
        self.epoch = 0
        self.ecnt = {}

    def next_segment(self):
        self.seg += 1

    def next_epoch(self):
        self.epoch += 1

    def new_dma_sem(self, name):
        k = ("dma", name)
        assert k not in self.dma_cnt
        self.dma_cnt[k] = 0
        self.sem_keys.append(k)
        return k

    def _deps(self, eng, reads, writes):
        sigs = []
        for r in reads:
            if r.last_w is not None:
                sigs.append(r.last_w)
        for w in writes:
            if w.last_w is not None:
                sigs.append(w.last_w)
            sigs.extend(w.reads)
        need = {}
        for (k, v, e) in sigs:
            if e == eng and k[0] == "eng" and (eng == "pe" or not SAME_ENGINE_SYNC):
                continue
            if self.waited[eng].get(k, 0) >= v:
                continue
            if need.get(k, 0) < v:
                need[k] = v
        for k, v in need.items():
            self.waited[eng][k] = v
        return list(need.items())

    def _mark(self, sig, reads, writes):
        for r in reads:
            if len(r.reads) > 64:
                best = {}
                for s in r.reads:
                    if s[0] not in best or best[s[0]][1] < s[1]:
                        best[s[0]] = s
                r.reads = list(best.values())
            r.reads.append(sig)
        for w in writes:
            w.last_w = sig
            w.reads = []

    def op(self, eng, fn, reads=(), writes=()):
        waits = self._deps(eng, reads, writes)
        k = ("eng", eng, self.epoch)
        if k not in self.ecnt:
            self.ecnt[k] = 0
            self.sem_keys.append(k)
        self.ecnt[k] += 1
        sig = (k, self.ecnt[k], eng)
        self.ops[eng].append((waits, fn, (k, 1), self.seg))
        self._mark(sig, reads, writes)

    def dma(self, eng, semk, fn, reads=(), writes=()):
        waits = self._deps(eng, reads, writes)
        self.dma_cnt[semk] += 16
        sig = (semk, self.dma_cnt[semk], "dma")
        self.ops[eng].append((waits, fn, (semk, 16), self.seg))
        self._mark(sig, reads, writes)
        return sig

    def wait_all(self, eng, sigs):
        for (k, v, e) in sigs:
            self.final_waits.append((eng, k, v))

    def emit(self):
        nc = self.nc
        with contextlib.ExitStack() as st:
            sems = {}
            for k in self.sem_keys:
                sems[k] = st.enter_context(nc.semaphore("s_" + "_".join(map(str, k))))
            for seg in range(self.seg + 1):
                last = seg == self.seg
                with nc.Block() as block:
                    hmap = {"pe": block.tensor, "act": block.scalar, "dve": block.vector,
                            "pool": block.gpsimd, "sp": block.sync}
                    for e in self.ENGS:
                        ops = [o for o in self.ops[e] if o[3] == seg]
                        fw = [(k, v) for (ee, k, v) in self.final_waits if ee == e] if last else []
                        if not ops and not fw:
                            continue

                        def body(h, ops=ops, fw=fw):
                            for (waits, fn, inc, _s) in ops:
                                for (k, v) in waits:
                                    h.wait_ge(sems[k], v)
                                ins = fn(h)
                                ins.then_inc(sems[inc[0]], inc[1])
                            for (k, v) in fw:
                                h.wait_ge(sems[k], v)
                        hmap[e](body)


def host_consts():
    h = np.arange(8, dtype=np.float64)
    lg = np.log1p(-np.exp2(-5.0 - h))
    i = np.arange(128, dtype=np.float64)
    dist = np.abs(i[None, :] - i[:, None])
    allowed = (np.arange(128)[:, None] // 64) <= (np.arange(128)[None, :] // 64)
    dmat = np.exp(lg[None, :, None] * dist[:, None, :]) * allowed[:, None, :]
    qdecm = np.broadcast_to(np.exp(lg[None, :, None] * (i[None, None, :] + 1.0)), (128, 8, 128))
    kdv = np.exp(lg[None, :] * (127.0 - i[:, None]))
    p = np.arange(128)
    cvec = np.zeros((128, 12), np.float64)
    cvec[:, 4:12] = kdv
    cvec[:, 0] = 10000.0 ** (-2.0 * (p % 64) / 128.0) / (2 * np.pi)
    cvec[:, 1] = np.where(p < 64, -1.0, 1.0)
    cvec[:, 2] = 10000.0 ** (-2.0 * (p % 32) / 64.0) / (2 * np.pi)
    cvec[:, 3] = np.where((p % 64) < 32, -1.0, 1.0)
    f = lambda a: np.ascontiguousarray(a.reshape(128, -1)).astype(np.float32)
    return {"dmat": f(dmat), "qdecm": f(qdecm), "cvec": cvec.astype(np.float32)}


_LG = np.log1p(-np.exp2(-5.0 - np.arange(8, dtype=np.float64)))
_CD = [float(np.exp(_LG[h] * 128.0)) for h in range(8)]


def build(nl, dbg=False):
    nc = bass.Bass("TRN2", target_bir_lowering=False)

    def din(name, shape, dt=F32):
        return nc.dram_tensor(name, list(shape), dt, kind="ExternalInput").ap()

    x_in = din("x", [S, D])
    cT_d = din("cT", [128, 16])
    pos_d = din("pos", [128, S], I32)
    dmat_d = din("dmat", [128, 1024]); qdecm_d = din("qdecm", [128, 1024])
    cvec_d = din("cvec", [128, 12])
    vecs_d = din("vecs", [nl, 128, NV])
    ada_d = din("ada_w", [nl, D, 3 * D])
    win_d = din("w_in", [nl, D, INW])
    lwa_d = din("lru_wa", [nl, 8, 128, 128]); lwx_d = din("lru_wx", [nl, 8, 128, 128])
    wuq_d = din("w_uq", [nl, 512, 1536]); wukv_d = din("w_ukv", [nl, 512, 2048])
    wbr_d = din("w_branch", [nl, 3072, D]); wout_d = din("w_out", [nl, D, D])
    out_d = nc.dram_tensor("out", [S, D], F32, kind="ExternalOutput").ap()
    xT = [nc.dram_tensor("xTa", [D, S], F32).ap(), nc.dram_tensor("xTb", [D, S], F32).ap()]
    ysc = nc.dram_tensor("ysc", [D, TT], F32).ap()

    P = Prog(nc)
    st = contextlib.ExitStack()

    def sb(name, shape, dt=F32):
        return st.enter_context(nc.sbuf_tensor("sb_" + name, list(shape), dt))

    identf = sb("identf", [128, 128]); identb = sb("identb", [128, 128], BF16)
    onesb = sb("onesb", [128, 128], BF16)
    dmat = sb("dmat", [128, 8, 128]); qdecm = sb("qdecm", [128, 8, 128])
    cvec = sb("cvec", [128, 12]); cact = sb("cact", [128, 16])
    rstdb = sb("rstdb", [128, TT]); qdb = [sb("qdb%d" % i, [128, TT], BF16) for i in range(2)]
    vecs = sb("vecs", [128, NV]); modv = sb("modv", [128, 48])
    gp = sb("gp", [128, 16]); rg1 = sb("rg1", [128, 16]); m8sp = sb("m8sp", [128, 8])
    posi = sb("posi", [128, TT], I32); posf = sb("posf", [128, TT])
    tab = sb("tab", [128, 4, TT])
    kvc = sb("kvc", [128, 4, S], BF16)
    krope = sb("krope", [64, S], BF16)
    Sst = sb("Sst", [128, 8, 128]); Sbf = sb("Sbf", [128, 8, 128], BF16)
    hst = sb("hst", [128, 8]); tail = sb("tail", [128, 8, 3])
    lwa = sb("lwa", [128, 8, 128], BF16); lwx = sb("lwx", [128, 8, 128], BF16)
    hT = sb("hT", [128, 16, TT], BF16)
    mrg = sb("mrg", [128, 16, TT], BF16)
    yT = sb("yT", [128, 8, TT], BF16)
    G = [sb("G%d" % i, [128, 8, TT], BF16) for i in range(4)]
    NF = 8
    Ft = [sb("F%d" % i, [128, TT]) for i in range(NF)]
    NB = 4
    Bt = [sb("B%d" % i, [128, TT], BF16) for i in range(NB)]
    xl = sb("xl", [128, TT + 3])
    lat = sb("lat", [128, 4, TT])
    xb = [sb("xb%d" % i, [128, TT]) for i in range(2)]
    yb = [sb("yb%d" % i, [128, TT]) for i in range(2)]
    NWB = 2
    wst = [sb("wst%d" % i, [128, 8 * 256]) for i in range(2)]
    tr_in, tr_out = wst[0], wst[1]
    wbf = [sb("wbf%d" % i, [128, 16 * 256], BF16) for i in range(NWB)]
    Q = [st.enter_context(nc.psum_tensor("Q%d" % i, [128, 512], F32)) for i in range(7)]
    QT = st.enter_context(nc.psum_tensor("QT", [128, 1024], BF16))

    R = {}

    def r(name):
        if name not in R:
            R[name] = Res(name)
        return R[name]

    rQ = [r("Q%d" % i) for i in range(7)]
    rQT = r("QT")

    class Pool_:
        def __init__(self, bufs, nm):
            self.bufs = bufs; self.i = 0; self.rs = [r("%s%d" % (nm, i)) for i in range(len(bufs))]

        def get(self):
            i = self.i; self.i = (i + 1) % len(self.bufs)
            return self.bufs[i], self.rs[i]
    Fp = Pool_(Ft, "F"); Bp = Pool_(Bt, "B")

    bank_i = [0]

    def bank(lo=0, hi=7):
        n = hi - lo
        i = lo + (bank_i[0] % n)
        bank_i[0] += 1
        return Q[i], rQ[i]

    ndma = [0]

    def dma_in(dst_ap, src_ap, wres, rres=(), eng="sp"):
        ndma[0] += 1
        k = P.new_dma_sem("d%d" % ndma[0])
        return P.dma(eng, k, lambda h: h.dma_start(out=dst_ap, in_=src_ap), reads=list(rres), writes=list(wres))

    sem_pool = {}

    def sdma(key, dst_ap, src_ap, wres, rres=(), eng="sp"):
        if key in ("w0", "w1"):
            key = key + "_e%d" % P.epoch
        if key not in sem_pool:
            sem_pool[key] = P.new_dma_sem(key)
        return P.dma(eng, sem_pool[key], lambda h: h.dma_start(out=dst_ap, in_=src_ap),
                     reads=list(rres), writes=list(wres))

    def act(out, in_, func, reads, writes, bias=None, scale=None):
        kw = {}
        if bias is not None:
            kw["bias"] = bias
        if scale is not None:
            kw["scale"] = scale
        P.op("act", lambda h: h.activation(out=out, in_=in_, func=func, **kw), reads=reads, writes=writes)

    def tt(eng, out, in0, in1, op, reads, writes):
        P.op(eng, lambda h: h.tensor_tensor(out=out, in0=in0, in1=in1, op=op), reads=reads, writes=writes)

    def ts(eng, out, in0, s1, s2, op0, op1, reads, writes):
        if op1 is None:
            P.op(eng, lambda h: h.tensor_scalar(out=out, in0=in0, scalar1=s1, scalar2=None, op0=op0), reads=reads, writes=writes)
        else:
            P.op(eng, lambda h: h.tensor_scalar(out=out, in0=in0, scalar1=s1, scalar2=s2, op0=op0, op1=op1), reads=reads, writes=writes)

    def stt(eng, out, in0, scalar, in1, op0, op1, reads, writes):
        P.op(eng, lambda h: h.scalar_tensor_tensor(out=out, in0=in0, scalar=scalar, in1=in1, op0=op0, op1=op1),
             reads=reads, writes=writes)

    def cp(eng, out, in_, reads, writes):
        P.op(eng, lambda h: h.tensor_copy(out=out, in_=in_), reads=reads, writes=writes)

    def mm(out, lhsT, rhs, start, stop, reads, writes):
        P.op("pe", lambda h: h.matmul(out, lhsT, rhs, start=start, stop=stop), reads=reads, writes=writes)

    def memset(eng, ap, val, writes):
        P.op(eng, lambda h: h.memset(ap, val), writes=writes)

    wcnt = [0]
    rwst = [r("tr_in"), r("tr_out")]
    rwbf = [r("wbf%d" % i) for i in range(NWB)]

    scnt = [0]

    def wload(src, KC, ncols, cast=True):
        if not cast:
            assert KC <= 8
            si = scnt[0] % 2; scnt[0] += 1
            sview = wst[si][:, 0:KC * ncols].rearrange("p (k c) -> p k c", k=KC)
            sdma("w%d" % si, sview, src.rearrange("(k p) c -> p k c", p=128), [rwst[si]])
            return (lambda k: sview[:, k, :]), rwst[si]
        u = wcnt[0]; wcnt[0] += 1
        bi = u % NWB
        bview = wbf[bi][:, 0:KC * ncols].rearrange("p (k c) -> p k c", k=KC)
        for k0 in range(0, KC, 8):
            kn = min(8, KC - k0)
            si = scnt[0] % 2; scnt[0] += 1
            sview = wst[si][:, 0:kn * ncols].rearrange("p (k c) -> p k c", k=kn)
            sdma("w%d" % si, sview, src[k0 * 128:(k0 + kn) * 128, :].rearrange("(k p) c -> p k c", p=128), [rwst[si]])
            cp("pool", wbf[bi][:, k0 * ncols:(k0 + kn) * ncols], wst[si][:, 0:kn * ncols], [rwst[si]], [rwbf[bi]])
        return (lambda k: bview[:, k, :]), rwbf[bi]

    def proj_fm(wv, wr, KC, c0, mcols, rhs_fn, rhs_res, lo=0, hi=7):
        ps, rps = bank(lo, hi)
        for k in range(KC):
            mm(ps[0:mcols, :], wv(k)[:, c0:c0 + mcols], rhs_fn(k), k == 0, k == KC - 1,
               [wr] + rhs_res, [rps])
        return ps, rps

    rhT = r("hT")
    hT_k = lambda k: hT[:, k, :]

    dma_in(dmat[:].rearrange("p a b -> p (a b)"), dmat_d, [r("dmat")])
    dma_in(qdecm[:].rearrange("p a b -> p (a b)"), qdecm_d, [r("qdecm")])
    dma_in(cvec[:], cvec_d, [r("cvec")])
    dma_in(cact[:], cT_d, [r("cact")])
    memset("pool", identf[:], 0.0, [r("identf")])
    P.op("pool", lambda h: h.affine_select(out=identf[:], in_=identf[:], pattern=[[-1, 128]],
                                           compare_op=ALU.not_equal, fill=1.0, base=0, channel_multiplier=1),
         reads=[r("identf")], writes=[r("identf")])
    cp("dve", identb[:], identf[:], [r("identf")], [r("identb")])
    memset("pool", onesb[:], 1.0, [r("onesb")])
    act(cact[:], cact[:], AF.Silu, [r("cact")], [r("cact")])

    for t in range(S // 128 if STAGE >= 0 else (0 if STAGE < -1 else NTDBG)):
        sdma("trin", tr_in[:], x_in[t * 128:(t + 1) * 128, :], [r("tr_in")])
        for g in range(4):
            ps, rps = bank()
            for j in range(4):
                c = g * 4 + j
                P.op("pe", lambda h, ps=ps, j=j, c=c: h.transpose(out=ps[:, j * 128:(j + 1) * 128],
                                                                   in_=tr_in[:, c * 128:(c + 1) * 128], identity=identf[:]),
                     reads=[r("tr_in"), r("identf")], writes=[rps])
            eng = "act" if g % 2 == 0 else "dve"
            if eng == "act":
                act(tr_out[:, g * 512:(g + 1) * 512], ps[:, :], AF.Copy, [rps], [r("tr_out")])
            else:
                cp("dve", tr_out[:, g * 512:(g + 1) * 512], ps[:, :], [rps], [r("tr_out")])
        sdma("trout", xT[0].rearrange("(c p) s -> p c s", p=128)[:, :, t * 128:(t + 1) * 128],
             tr_out[:].rearrange("p (c s) -> p c s", c=16), [r("xT0_%d" % (t // 4))], [r("tr_out")])

    def rope_tables(p):
        t0 = p * TT
        sdma("pos", posi[:], pos_d[:, t0:t0 + TT], [r("posi")])
        cp("dve", posf[:], posi[:], [r("posi")], [r("posf")])
        for ti, (fcol, scol, phase, npart) in enumerate([(0, None, 0.25, 128), (0, 1, 0.0, 128),
                                                         (2, None, 0.25, 64), (2, 3, 0.0, 64)]):
            u, ru = Fp.get()
            ts("dve", u[0:npart, :], posf[0:npart, :], cvec[0:npart, fcol:fcol + 1], phase + 0.5, ALU.mult, ALU.add,
               [r("posf"), r("cvec")], [ru])
            ki = posi
            cp("dve", ki[0:npart, :], u[0:npart, :], [ru], [r("posi")])
            kf, rkf = Fp.get()
            cp("dve", kf[0:npart, :], ki[0:npart, :], [r("posi")], [rkf])
            tt("dve", u[0:npart, :], u[0:npart, :], kf[0:npart, :], ALU.subtract, [ru, rkf], [ru])
            ts("dve", kf[0:npart, :], u[0:npart, :], 0.0, None, ALU.is_lt, None, [ru], [rkf])
            tt("dve", u[0:npart, :], u[0:npart, :], kf[0:npart, :], ALU.add, [ru, rkf], [ru])
            act(tab[0:npart, ti, :], u[0:npart, :], AF.Sin, [ru, r("negpi")], [r("tab")], bias=negpi[0:npart, :], scale=6.283185)
            if scol is not None:
                ts("dve", tab[0:npart, ti, :], tab[0:npart, ti, :], cvec[0:npart, scol:scol + 1], None, ALU.mult, None,
                   [r("tab"), r("cvec")], [r("tab")])

    negpi = sb("negpi", [128, 1])
    memset("pool", negpi[:], -3.1415925, [r("negpi")])
    epsb = sb("epsb", [128, 1])
    memset("pool", epsb[:], EPS, [r("epsb")])

    def rstd_from(ps, rps, inv_n):
        sd, rsd = rstdb, r("rstdb")
        act(sd[:], ps[:, :], AF.Sqrt, [rps, r("epsb")], [rsd], bias=epsb[:], scale=inv_n)
        P.op("dve", lambda h: h.reciprocal(out=sd[:], in_=sd[:]), reads=[rsd], writes=[rsd])
        return sd, rsd

    def layer_prologue(l):
        sdma("vecs", vecs[:], vecs_d[l], [r("vecs")])
        psm, rpsm = Q[6], rQ[6]
        for u in range(24):
            for kh in range(2):
                wv, wr = wload(ada_d[l, kh * 1024:(kh + 1) * 1024, u * 256:(u + 1) * 256], 8, 256, cast=False)
                for j in range(2):
                    m = kh * 48 + u * 2 + j
                    for k in range(8):
                        mm(psm[:, m:m + 1], wv(k)[:, j * 128:(j + 1) * 128], cact[:, kh * 8 + k:kh * 8 + k + 1], k == 0, k == 7,
                           [wr, r("cact")], [rpsm])
        tt("dve", modv[:], psm[:, 0:48], vecs[:, 0:48], ALU.add, [rpsm, r("vecs")], [r("modv")])
        tt("dve", modv[:], psm[:, 48:96], modv[:], ALU.add, [rpsm, r("modv")], [r("modv")])
        stt("dve", gp[:], modv[:, 16:32], 1.0, vecs[:, 48:64], ALU.add, ALU.mult, [r("modv"), r("vecs")], [r("gp")])
        stt("dve", rg1[:], modv[:, 32:48], 1.0, vecs[:, 64:80], ALU.add, ALU.mult, [r("modv"), r("vecs")], [r("rg1")])
        act(m8sp[:], vecs[:, 144:152], AF.Exp, [r("vecs")], [r("m8sp")], scale=-1.0)
        ts("dve", m8sp[:], m8sp[:], 1.0, None, ALU.add, None, [r("m8sp")], [r("m8sp")])
        act(m8sp[:], m8sp[:], AF.Ln, [r("m8sp")], [r("m8sp")])
        ts("dve", m8sp[:], m8sp[:], -8.0, None, ALU.mult, None, [r("m8sp")], [r("m8sp")])
        for (src, dst, nm) in ((lwa_d, lwa, "lwa"), (lwx_d, lwx, "lwx")):
            si = scnt[0] % 2; scnt[0] += 1
            sview = wst[si][:, 0:1024].rearrange("p (n d) -> p n d", n=8)
            sdma("w%d" % si, sview, src[l].rearrange("n c d -> c n d"), [rwst[si]])
            cp("pool", dst[:].rearrange("p n d -> p (n d)"), wst[si][:, 0:1024], [rwst[si]], [r(nm)])
        memset("pool", Sst[:], 0.0, [r("Sst")])
        memset("pool", Sbf[:], 0.0, [r("Sbf")])
        memset("pool", hst[:], 0.0, [r("hst")])
        memset("pool", tail[:], 0.0, [r("tail")])

    def branch_merge(l, br):
        for u in range(8):
            wbv, wbr_ = wload(wbr_d[l, br * 1024:(br + 1) * 1024, u * 256:(u + 1) * 256], 8, 256)
            gc0 = 8256 + br * 2048 + u * 256
            wgv, wgr = wload(win_d[l, :, gc0:gc0 + 256], 16, 256)
            for j in range(2):
                f = u * 2 + j
                psz, rpz = proj_fm(wbv, wbr_, 8, j * 128, 128, lambda k: yT[:, k, :], [r("yT")])
                psg, rpg = proj_fm(wgv, wgr, 16, j * 128, 128, hT_k, [rhT])
                sg, rsg = Fp.get()
                act(sg[:], psg[:, :], AF.Sigmoid, [rpg], [rsg])
                if br == 0:
                    tt("dve", mrg[:, f, :], sg[:], psz[:, :], ALU.mult, [rsg, rpz], [r("mrg%d" % f)])
                else:
                    tt("dve", sg[:], sg[:], psz[:, :], ALU.mult, [rsg, rpz], [rsg])
                    tt("dve", mrg[:, f, :], mrg[:, f, :], sg[:], ALU.add, [rsg, r("mrg%d" % f)], [r("mrg%d" % f)])

    def gate_mul(l, c_base):
        for u in range(4):
            wv, wr = wload(win_d[l, :, c_base + u * 256:c_base + (u + 1) * 256], 16, 256)
            for j in range(2):
                m = u * 2 + j
                ps, rps = proj_fm(wv, wr, 16, j * 128, 128, hT_k, [rhT])
                sg, rsg = Fp.get()
                act(sg[:], ps[:, :], AF.Silu, [rps], [rsg])
                tt("dve", yT[:, m, :], yT[:, m, :], sg[:], ALU.mult, [rsg, r("yT")], [r("yT")])

    def rope_evac(ps, rps, npart, half, ci, si, out_ap, out_res, scale):
        raw, rraw = Fp.get()
        act(raw[0:npart, :], ps[0:npart, :], AF.Copy, [rps], [rraw])
        sw, rsw = Fp.get()
        cp("dve", sw[0:half, :], raw[half:npart, :], [rraw], [rsw])
        cp("dve", sw[half:npart, :], raw[0:half, :], [rraw], [rsw])
        t1, rt1 = Fp.get()
        tt("dve", t1[0:npart, :], raw[0:npart, :], tab[0:npart, ci, :], ALU.mult, [rraw, r("tab")], [rt1])
        tt("dve", sw[0:npart, :], sw[0:npart, :], tab[0:npart, si, :], ALU.mult, [rsw, r("tab")], [rsw])
        tt("dve", t1[0:npart, :], t1[0:npart, :], sw[0:npart, :], ALU.add, [rt1, rsw], [rt1])
        act(out_ap, t1[0:npart, :], AF.Copy, [rt1], out_res, scale=scale)

    def run_pass(l, p, src, dst, last):
        t0 = p * TT
        srcv = src.rearrange("(c p) s -> p c s", p=128)
        dstv = dst.rearrange("(c p) s -> p c s", p=128)
        rsrc = r("xT%d_%d" % (l % 2, p)); rdst = r("xT%d_%d" % ((l + 1) % 2, p))
        if STAGE < 2:
            return
        rope_tables(p)
        if STAGE < 3:
            return
        pst, rpst = Q[6], rQ[6]
        for c in range(16):
            sdma("xb%d" % (c % 2), xb[c % 2][:], srcv[:, c, t0:t0 + TT], [r("xb%d" % (c % 2))], [rsrc])
            sq, rsq = Bp.get()
            act(sq[:], xb[c % 2][:], AF.Square, [r("xb%d" % (c % 2))], [rsq])
            mm(pst[:, :], onesb[:], sq[:], c == 0, c == 15, [r("onesb"), rsq], [rpst])
        rstd, rrstd = rstd_from(pst, rpst, 1.0 / D)
        for c in range(16):
            sdma("xb%d" % (c % 2), xb[c % 2][:], srcv[:, c, t0:t0 + TT], [r("xb%d" % (c % 2))], [rsrc])
            tmp, rtmp = Fp.get()
            stt("dve", tmp[:], xb[c % 2][:], gp[:, c:c + 1], rstd[:], ALU.mult, ALU.mult,
                [r("xb%d" % (c % 2)), r("gp"), rrstd], [rtmp])
            act(hT[:, c, :], tmp[:], AF.Identity, [rtmp, r("modv")], [rhT], bias=modv[:, c:c + 1])

        if STAGE < 4:
            return
        qT, kT, vtm, kdec = G[0], G[1], G[2], G[3]
        for u in range(8):
            wv, wr = wload(win_d[l, :, u * 256:(u + 1) * 256], 16, 256)
            for j in range(2):
                hh = (u % 4) * 2 + j
                ps, rps = proj_fm(wv, wr, 16, j * 128, 128, hT_k, [rhT])
                dstbuf = qT if u < 4 else kT
                rope_evac(ps, rps, 128, 64, 0, 1, dstbuf[:, hh, :], [r("G0" if u < 4 else "G1")],
                          (128.0 ** -0.5) if u < 4 else 1.0)
        if STAGE < 4.2:
            return
        for u in range(4):
            wv, wr = wload(win_d[l, :, 2048 + u * 256:2048 + (u + 1) * 256], 16, 256)
            for ti in range(4):
                ps, rps = bank()
                for k in range(16):
                    mm(ps[:, 0:256], hT[:, k, ti * 128:(ti + 1) * 128], wv(k), k == 0, k == 15, [wr, rhT], [rps])
                if ti % 2 == 0:
                    act(vtm[:].rearrange("p a b -> p (a b)")[:, ti * 1024 + u * 256: ti * 1024 + (u + 1) * 256],
                        ps[:, 0:256], AF.Copy, [rps], [r("G2")])
                else:
                    cp("dve", vtm[:].rearrange("p a b -> p (a b)")[:, ti * 1024 + u * 256: ti * 1024 + (u + 1) * 256],
                       ps[:, 0:256], [rps], [r("G2")])
        if STAGE < 4.4:
            return
        vflat = vtm[:].rearrange("p a b -> p (a b)")
        kdflat = kdec[:].rearrange("p a b -> p (a b)")
        for ti in range(4):
            tsl = slice(ti * 128, (ti + 1) * 128)
            for hh in range(8):
                P.op("pe", lambda h, hh=hh, tsl=tsl: h.transpose(out=QT[:, hh * 128:(hh + 1) * 128], in_=kT[:, hh, tsl],
                                                                identity=identb[:]),
                     reads=[r("G1"), r("identb")], writes=[rQT])
            for hh in range(8):
                ts("dve" if hh % 2 == 0 else "dve", kdflat[:, ti * 1024 + hh * 128:ti * 1024 + (hh + 1) * 128],
                   QT[:, hh * 128:(hh + 1) * 128], cvec[:, 4 + hh:5 + hh], None, ALU.mult, None, [rQT, r("cvec")], [r("G3")])
            qd, rqd = qdb[0], r("qdb0")
            qd2, rqd2 = qdb[1], r("qdb1")
            tt("pool", qd[:].rearrange("p (a b) -> p a b", a=4), qT[:, 0:4, tsl], qdecm[:, 0:4, :], ALU.mult,
               [r("G0"), r("qdecm")], [rqd])
            tt("pool", qd2[:].rearrange("p (a b) -> p a b", a=4), qT[:, 4:8, tsl], qdecm[:, 4:8, :], ALU.mult,
               [r("G0"), r("qdecm")], [rqd2])
            for half in range(2):
                h0 = half * 4
                psS, rS = Q[0], rQ[0]
                psO, rO = Q[1], rQ[1]
                psK, rK = Q[2], rQ[2]
                for hq in range(4):
                    hh = h0 + hq
                    mm(psS[:, hq * 128:(hq + 1) * 128], kT[:, hh, tsl], qT[:, hh, tsl], True, True,
                       [r("G0"), r("G1")], [rS])
                pt, rpt = Bp.get()
                tt("dve", pt[:], psS[:, :], dmat[:].rearrange("p a b -> p (a b)")[:, h0 * 128:(h0 + 4) * 128], ALU.mult,
                   [rS, r("dmat")], [rpt])
                qdh = qd if half == 0 else qd2
                rqdh = rqd if half == 0 else rqd2
                for hq in range(4):
                    hh = h0 + hq
                    vcol = slice(ti * 1024 + hh * 128, ti * 1024 + (hh + 1) * 128)
                    mm(psO[:, hq * 128:(hq + 1) * 128], vflat[:, vcol], pt[:, hq * 128:(hq + 1) * 128], True, False,
                       [r("G2"), rpt], [rO])
                    mm(psO[:, hq * 128:(hq + 1) * 128], Sbf[:, hh, :], qdh[:, hq * 128:(hq + 1) * 128], False, True,
                       [r("Sbf"), rqdh], [rO])
                for hq in range(4):
                    hh = h0 + hq
                    vcol = slice(ti * 1024 + hh * 128, ti * 1024 + (hh + 1) * 128)
                    mm(psK[:, hq * 128:(hq + 1) * 128], kdflat[:, vcol], vflat[:, vcol], True, True,
                       [r("G2"), r("G3")], [rK])
                ob, rob = Bp.get()
                act(ob[:], psO[:, :], AF.Copy, [rO], [rob])
                osq, rosq = Bp.get()
                act(osq[:], psO[:, :], AF.Square, [rO], [rosq])
                ps1, r1 = Q[3], rQ[3]
                ps2, r2 = Q[4], rQ[4]
                mm(ps1[:, :], onesb[:], ob[:], True, True, [r("onesb"), rob], [r1])
                mm(ps2[:, :], onesb[:], osq[:], True, True, [r("onesb"), rosq], [r2])
                mean, rmean = Fp.get()
                act(mean[:], ps1[:, :], AF.Copy, [r1], [rmean], scale=1.0 / 128)
                var, rvar = Fp.get()
                tt("dve", var[:], mean[:], mean[:], ALU.mult, [rmean], [rvar])
                stt("dve", var[:], ps2[:, :], 1.0 / 128, var[:], ALU.mult, ALU.subtract, [r2, rvar], [rvar])
                act(var[:], var[:], AF.Sqrt, [rvar, r("epsb")], [rvar], bias=epsb[:])
                P.op("dve", lambda h, var=var: h.reciprocal(out=var[:], in_=var[:]), reads=[rvar], writes=[rvar])
                tt("dve", mean[:], ob[:], mean[:], ALU.subtract, [rob, rmean], [rmean])
                tt("dve", mean[:], mean[:], var[:], ALU.mult, [rmean, rvar], [rmean])
                for hq in range(4):
                    hh = h0 + hq
                    ts("dve", yT[:, hh, tsl], mean[:, hq * 128:(hq + 1) * 128], vecs[:, 80 + hh:81 + hh], None,
                       ALU.mult, None, [rmean, r("vecs")], [r("yT")])
                for hq in range(4):
                    hh = h0 + hq
                    stt("dve", Sst[:, hh, :], Sst[:, hh, :], _CD[hh], psK[:, hq * 128:(hq + 1) * 128], ALU.mult, ALU.add,
                        [r("Sst"), rK], [r("Sst")])
                ssl = Sst[:].rearrange("p a b -> p (a b)")[:, h0 * 128:(h0 + 4) * 128]
                act(Sbf[:].rearrange("p a b -> p (a b)")[:, h0 * 128:(h0 + 4) * 128], ssl, AF.Copy, [r("Sst")], [r("Sbf")])
        if STAGE < 5:
            return
        gate_mul(l, 3072)
        branch_merge(l, 0)
        if STAGE < 6:
            return

        for n in range(8):
            if n % 2 == 0:
                wv, wr = wload(win_d[l, :, 4096 + (n // 2) * 256:4096 + (n // 2 + 1) * 256], 16, 256)
            ps, rps = proj_fm(wv, wr, 16, (n % 2) * 128, 128, hT_k, [rhT])
            cp("dve", xl[:, 0:3], tail[:, n, :], [r("tail")], [r("xl")])
            act(xl[:, 3:3 + TT], ps[:, :], AF.Copy, [rps], [r("xl")])
            cp("dve", tail[:, n, :], xl[:, TT:TT + 3], [r("xl")], [r("tail")])
            xc, rxc = Fp.get()
            cw = lambda kk: vecs[:, 88 + n * 4 + kk:89 + n * 4 + kk]
            act(xc[:], xl[:, 3:3 + TT], AF.Identity, [r("xl"), r("vecs")], [rxc], bias=vecs[:, 120 + n:121 + n], scale=cw(3))
            for kk in range(3):
                stt("dve", xc[:], xl[:, kk:kk + TT], cw(kk), xc[:], ALU.mult, ALU.add, [r("xl"), r("vecs"), rxc], [rxc])
            xcb, rxcb = Bp.get()
            act(xcb[:], xc[:], AF.Copy, [rxc], [rxcb])
            psr, rpr = bank()
            mm(psr[:, :], lwa[:, n, :], xcb[:], True, True, [r("lwa"), rxcb], [rpr])
            psi, rpi = bank()
            mm(psi[:, :], lwx[:, n, :], xcb[:], True, True, [r("lwx"), rxcb], [rpi])
            a, ra = Fp.get()
            act(a[:], psr[:, :], AF.Sigmoid, [rpr, r("vecs")], [ra], bias=vecs[:, 128 + n:129 + n])
            ig, rig = Fp.get()
            act(ig[:], psi[:, :], AF.Sigmoid, [rpi, r("vecs")], [rig], bias=vecs[:, 136 + n:137 + n])
            act(a[:], a[:], AF.Exp, [ra, r("m8sp")], [ra], scale=m8sp[:, n:n + 1])
            bb, rbb = Fp.get()
            tt("dve", bb[:], a[:], a[:], ALU.mult, [ra], [rbb])
            ts("dve", bb[:], bb[:], -1.0, 1.0, ALU.mult, ALU.add, [rbb], [rbb])
            act(bb[:], bb[:], AF.Sqrt, [rbb], [rbb])
            tt("dve", ig[:], ig[:], xc[:], ALU.mult, [rig, rxc], [rig])
            tt("dve", bb[:], bb[:], ig[:], ALU.mult, [rbb, rig], [rbb])
            hs, rhs = Fp.get()
            P.op("dve", lambda h, hs=hs, a=a, bb=bb, n=n: h.tensor_tensor_scan(out=hs[:], data0=a[:], data1=bb[:],
                                                                            initial=hst[:, n:n + 1], op0=ALU.mult, op1=ALU.add),
                 reads=[ra, rbb, r("hst")], writes=[rhs])
            cp("dve", hst[:, n:n + 1], hs[:, TT - 1:TT], [rhs], [r("hst")])
            act(yT[:, n, :], hs[:], AF.Copy, [rhs], [r("yT")])
        gate_mul(l, 5120)
        branch_merge(l, 1)
        if STAGE < 7:
            return

        qlatn, qnope, qrp = G[0], G[1], G[2]
        for which in range(2):
            pst, rpst = Q[6], rQ[6]
            for u in range(2):
                c0 = 6144 + which * 512 + u * 256
                wv, wr = wload(win_d[l, :, c0:c0 + 256], 16, 256)
                for j in range(2):
                    c = u * 2 + j
                    ps, rps = proj_fm(wv, wr, 16, j * 128, 128, hT_k, [rhT], 0, 6)
                    act(lat[:, c, :], ps[:, :], AF.Copy, [rps], [r("lat")])
                    sq, rsq = Bp.get()
                    act(sq[:], ps[:, :], AF.Square, [rps], [rsq])
                    mm(pst[:, :], onesb[:], sq[:], c == 0, c == 3, [r("onesb"), rsq], [rpst])
            rstd, rrstd = rstd_from(pst, rpst, 1.0 / 512)
            for c in range(4):
                gcol = 152 + which * 4 + c
                if which == 0:
                    stt("dve", qlatn[:, c, :], lat[:, c, :], vecs[:, gcol:gcol + 1], rstd[:], ALU.mult, ALU.mult,
                        [r("lat"), r("vecs"), rrstd], [r("G0")])
                else:
                    stt("dve", kvc[:, c, t0:t0 + TT], lat[:, c, :], vecs[:, gcol:gcol + 1], rstd[:], ALU.mult, ALU.mult,
                        [r("lat"), r("vecs"), rrstd], [r("kvc")])
        wv, wr = wload(win_d[l, :, 7168:7232], 16, 64)
        ps, rps = proj_fm(wv, wr, 16, 0, 64, hT_k, [rhT])
        rope_evac(ps, rps, 64, 32, 2, 3, krope[:, t0:t0 + TT], [r("krope")], 1.0)
        sc = 192.0 ** -0.5
        for u in range(6):
            wv, wr = wload(wuq_d[l, :, u * 256:(u + 1) * 256], 4, 256)
            if u < 4:
                for j in range(2):
                    hh = u * 2 + j
                    ps, rps = proj_fm(wv, wr, 4, j * 128, 128, lambda k: qlatn[:, k, :], [r("G0")])
                    act(qnope[:, hh, :], ps[:, :], AF.Copy, [rps], [r("G1")], scale=sc)
            else:
                for j in range(4):
                    hh = (u - 4) * 4 + j
                    ps, rps = proj_fm(wv, wr, 4, j * 64, 64, lambda k: qlatn[:, k, :], [r("G0")])
                    rope_evac(ps, rps, 64, 32, 2, 3, qrp[0:64, hh, :], [r("G2")], sc)
        nkt = (t0 + TT) // 128
        nkg = (t0 + TT) // 512
        Kh = G[3]
        Kflat = Kh[:].rearrange("p a b -> p (a b)")
        for hh in range(8):
            wv, wr = wload(wukv_d[l, :, hh * 256:(hh + 1) * 256], 4, 256)
            for g in range(nkg):
                ps, rps = bank(2, 6)
                for k in range(4):
                    mm(ps[:, :], wv(k)[:, 0:128], kvc[:, k, g * 512:(g + 1) * 512], k == 0, k == 3, [wr, r("kvc")], [rps])
                act(Kflat[:, g * 512:(g + 1) * 512], ps[:, :], AF.Copy, [rps], [r("G3")])
                ps, rps = bank(2, 6)
                for kt4 in range(4):
                    kt = g * 4 + kt4
                    for k in range(4):
                        mm(ps[:, kt4 * 128:(kt4 + 1) * 128], kvc[:, k, kt * 128:(kt + 1) * 128], wv(k)[:, 128:256],
                           k == 0, k == 3, [wr, r("kvc")], [rps])
                cp("dve", Kflat[:, 2048 + g * 512:2048 + (g + 1) * 512], ps[:, :], [rps], [r("G3")])
            psO, rO = Q[0], rQ[0]
            psD, rD = Q[1], rQ[1]
            for kt in range(nkt):
                jl = kt - t0 // 128
                c0 = max(jl, 0) * 128
                psS, rS = bank(2, 6)
                mm(psS[:, c0:TT], Kflat[:, kt * 128:(kt + 1) * 128], qnope[:, hh, c0:TT], True, False,
                   [r("G3"), r("G1")], [rS])
                mm(psS[:, c0:TT], krope[0:64, kt * 128:(kt + 1) * 128], qrp[0:64, hh, c0:TT], False, True,
                   [r("krope"), r("G2")], [rS])
                pt, rpt = Bp.get()
                act(pt[:, c0:TT], psS[:, c0:TT], AF.Exp, [rS], [rpt])
                if jl >= 0:
                    memset("pool", pt[64:128, c0:c0 + 64], 0.0, [rpt])
                mm(psO[:, c0:TT], Kflat[:, 2048 + kt * 128:2048 + (kt + 1) * 128], pt[:, c0:TT], kt == 0, kt == nkt - 1,
                   [r("G3"), rpt], [rO])
                mm(psD[:, c0:TT], onesb[:], pt[:, c0:TT], kt == 0, kt == nkt - 1, [r("onesb"), rpt], [rD])
            rd, rrd = Fp.get()
            P.op("dve", lambda h, rd=rd, psD=psD: h.reciprocal(out=rd[:], in_=psD[:, :]), reads=[rD], writes=[rrd])
            tt("dve", yT[:, hh, :], psO[:, :], rd[:], ALU.mult, [rO, rrd], [r("yT")])
        gate_mul(l, 7232)
        branch_merge(l, 2)
        if STAGE < 8:
            return

        pst, rpst = Q[6], rQ[6]
        for u in range(8):
            wv, wr = wload(wout_d[l, :, u * 256:(u + 1) * 256], 16, 256)
            for j in range(2):
                f = u * 2 + j
                ps, rps = proj_fm(wv, wr, 16, j * 128, 128, lambda k: mrg[:, k, :], [r("mrg%d" % k) for k in range(16)], 0, 6)
                act(yb[f % 2][:], ps[:, :], AF.Copy, [rps], [r("yb%d" % (f % 2))])
                sq, rsq = Bp.get()
                act(sq[:], ps[:, :], AF.Square, [rps], [rsq])
                mm(pst[:, :], onesb[:], sq[:], f == 0, f == 15, [r("onesb"), rsq], [rpst])
                sdma("ysc%d" % (f % 2), ysc[f * 128:(f + 1) * 128, :], yb[f % 2][:], [r("ysc%d" % f)], [r("yb%d" % (f % 2))])
        rstd, rrstd = rstd_from(pst, rpst, 1.0 / D)
        for f in range(16):
            sdma("xb%d" % (f % 2), xb[f % 2][:], srcv[:, f, t0:t0 + TT], [r("xb%d" % (f % 2))], [rsrc])
            sdma("ysc%d" % (f % 2), yb[f % 2][:], ysc[f * 128:(f + 1) * 128, :], [r("yb%d" % (f % 2))], [r("ysc%d" % f)])
            tmp, rtmp = Fp.get()
            stt("dve", tmp[:], yb[f % 2][:], rg1[:, f:f + 1], rstd[:], ALU.mult, ALU.mult,
                [r("yb%d" % (f % 2)), r("rg1"), rrstd], [rtmp])
            tt("dve", xb[f % 2][:], tmp[:], xb[f % 2][:], ALU.add, [rtmp, r("xb%d" % (f % 2))], [r("xb%d" % (f % 2))])
            sdma("xo%d" % (f % 2), dstv[:, f, t0:t0 + TT], xb[f % 2][:], [rdst], [r("xb%d" % (f % 2))])

    for l in range(nl):
        if l > 0:
            P.next_epoch()
        if STAGE >= 1:
            layer_prologue(l)
        for p in (range(NPASS) if NPASS_RUN is None else (NPASS_RUN if isinstance(NPASS_RUN, (list, tuple)) else range(NPASS_RUN))):
            run_pass(l, p, xT[l % 2], xT[(l + 1) % 2], l == nl - 1)
            P.next_segment()

    fin = xT[nl % 2 if STAGE >= 8 else 0].rearrange("(c p) s -> p c s", p=128)
    sigs = []
    for t in range(S // 128 if STAGE >= 0 else NTDBG):
        sdma("trin", tr_in[:].rearrange("p (c s) -> p c s", c=16), fin[:, :, t * 128:(t + 1) * 128],
             [r("tr_in")], [r("xT%d_%d" % (nl % 2 if STAGE >= 8 else 0, t // 4))])
        for g in range(4):
            ps, rps = bank()
            for j in range(4):
                c = g * 4 + j
                P.op("pe", lambda h, ps=ps, j=j, c=c: h.transpose(out=ps[:, j * 128:(j + 1) * 128],
                                                                   in_=tr_in[:, c * 128:(c + 1) * 128], identity=identf[:]),
                     reads=[r("tr_in"), r("identf")], writes=[rps])
            if g % 2 == 0:
                act(tr_out[:, g * 512:(g + 1) * 512], ps[:, :], AF.Copy, [rps], [r("tr_out")])
            else:
                cp("dve", tr_out[:, g * 512:(g + 1) * 512], ps[:, :], [rps], [r("tr_out")])
        sigs.append(sdma("trout", out_d[t * 128:(t + 1) * 128, :], tr_out[:], [r("outd")], [r("tr_out")]))
    P.wait_all("sp", sigs[-1:])
    P.emit()
    st.close()
    return nc


def _vecs(l, ada_b, norm_pre, norm_post, ret_gn, conv_w, conv_b, ba, bx, lam, qn, kvn):
    v = np.zeros((128, NV), np.float32)
    v[:, 0:48] = ada_b[l].reshape(48, 128).T
    v[:, 48:64] = norm_pre[l].reshape(16, 128).T
    v[:, 64:80] = norm_post[l].reshape(16, 128).T
    v[:, 80:88] = ret_gn[l].reshape(8, 128).T
    v[:, 88:120] = conv_w[l].reshape(4, 8, 128).transpose(2, 1, 0).reshape(128, 32)
    v[:, 120:128] = conv_b[l].reshape(8, 128).T
    v[:, 128:136] = ba[l].reshape(8, 128).T
    v[:, 136:144] = bx[l].reshape(8, 128).T
    v[:, 144:152] = lam[l].reshape(8, 128).T
    v[:, 152:156] = qn[l].reshape(4, 128).T
    v[:, 156:160] = kvn[l].reshape(4, 128).T
    return v


_UQ_PERM = np.concatenate([np.concatenate([np.arange(h * 192, h * 192 + 128) for h in range(8)]),
                           np.concatenate([np.arange(h * 192 + 128, h * 192 + 192) for h in range(8)])])

N_CORES = 4
LAYERS_PER_LAUNCH = 2


def kernel(x, c, positions, ada_w, ada_b, norm_pre, norm_post, w_in, ret_gn,
           lru_conv_w, lru_conv_b, lru_wa, lru_ba, lru_wx, lru_bx, lru_lambda,
           mla_q_norm, mla_w_uq, mla_kv_norm, mla_w_ukv, w_branch, w_out):
    x = np.asarray(x, np.float32)
    hc = host_consts()
    B = x.shape[0]
    nl = LAYERS_PER_LAUNCH
    nc = build(nl)
    cur = [np.ascontiguousarray(x[b]) for b in range(B)]
    f32 = lambda a: np.ascontiguousarray(np.asarray(a, np.float32))
    for l0 in range(0, DEPTH, nl):
        ls = list(range(l0, l0 + nl))
        vecs = np.stack([_vecs(l, *[np.asarray(a) for a in (ada_b, norm_pre, norm_post, ret_gn, lru_conv_w, lru_conv_b,
                                                           lru_ba, lru_bx, lru_lambda, mla_q_norm, mla_kv_norm)]) for l in ls])
        shared = {
            "vecs": vecs, "ada_w": f32(np.asarray(ada_w)[ls]), "w_in": f32(np.asarray(w_in)[ls]),
            "lru_wa": f32(np.asarray(lru_wa)[ls]), "lru_wx": f32(np.asarray(lru_wx)[ls]),
            "w_uq": f32(np.asarray(mla_w_uq)[ls][:, :, _UQ_PERM]), "w_ukv": f32(np.asarray(mla_w_ukv)[ls]),
            "w_branch": f32(np.asarray(w_branch)[ls]), "w_out": f32(np.asarray(w_out)[ls]),
        }
        shared.update(hc)
        in_maps = []
        for b in range(N_CORES):
            m = dict(shared)
            m["x"] = cur[b]
            m["cT"] = f32(np.asarray(c)[b].reshape(16, 128).T)
            m["pos"] = np.ascontiguousarray(np.broadcast_to(np.asarray(positions)[b].astype(np.int32)[None, :], (128, S)))
            in_maps.append(m)
        res = run_bass_kernel_spmd(nc, in_maps, core_ids=list(range(N_CORES)))
        cur = [np.asarray(res.results[b]["out"]) for b in range(N_CORES)]
    return np.stack(cur).astype(np.float32)
```
